# Optimizing a Trainium2 kernel written in Bass

```python
import functools
import jax, jax.numpy as jnp
from jax import lax
import numpy as np


D_MODEL = 1024
BATCH = 2
SEQ = 16384
DEPTH = 2

N_MIXERS = 2
N_SUB = 3
EPS = 1e-6

HG_KDIM = 128
HG_HEADS = D_MODEL // HG_KDIM
HG_VDIM = D_MODEL // HG_HEADS
HG_F = HG_HEADS * HG_KDIM
HG_CHUNK = 64

GM_FFN = 6 * D_MODEL
GM_HALF = GM_FFN // 2
GM_GROUPS = 8
GM_GDIM = GM_HALF // GM_GROUPS
GM_CHUNK = 128

D_FF = ((8 * D_MODEL // 3 + 127) // 128) * 128

N_HGRN = (DEPTH + 1) // 2
N_GMLP = DEPTH // 2

kernel_name = "hybrid_hgrn2_gmlp_macaron_adaln"


def rms_norm(h, g):
    hf = h.astype(jnp.float32)
    y = hf * lax.rsqrt(jnp.mean(hf * hf, axis=-1, keepdims=True) + EPS)
    return (y * g.astype(jnp.float32)).astype(h.dtype)


def layer_norm(h, g, b):
    hf = h.astype(jnp.float32)
    mu = jnp.mean(hf, axis=-1, keepdims=True)
    d = hf - mu
    y = d * lax.rsqrt(jnp.mean(d * d, axis=-1, keepdims=True) + EPS)
    return (y * g.astype(jnp.float32) + b.astype(jnp.float32)).astype(h.dtype)


def swiglu(h, w_in, w_out):
    a, b = jnp.split(h @ w_in, 2, axis=-1)
    return (jax.nn.silu(a) * b) @ w_out


def hgrn2_mix(h, w_in, w_out, out_norm, lb):
    B, S, _ = h.shape
    f32 = jnp.float32
    proj = h @ w_in
    q, f, i, g = jnp.split(proj, [HG_F, 2 * HG_F, 2 * HG_F + D_MODEL], axis=-1)
    q = jax.nn.silu(q.astype(f32))
    fx = f.astype(f32)
    log_f = jnp.logaddexp(jnp.log(lb), jnp.log1p(-lb) + jax.nn.log_sigmoid(fx))
    k = (1.0 - lb) * jax.nn.sigmoid(-fx)
    nc = S // HG_CHUNK

    def to_chunks(t, d):
        return t.reshape(B, nc, HG_CHUNK, HG_HEADS, d).transpose(1, 0, 3, 2, 4)

    qc = to_chunks(q, HG_KDIM)
    kc = to_chunks(k, HG_KDIM)
    vc = to_chunks(i.astype(f32), HG_VDIM)
    bc = jnp.cumsum(to_chunks(log_f, HG_KDIM), axis=3)
    causal = jnp.tril(jnp.ones((HG_CHUNK, HG_CHUNK), bool))[:, :, None]

    def step(state, inp):
        qb, kb, vb, bb = inp
        o_inter = jnp.einsum('bhtk,bhkv->bhtv', qb * jnp.exp(bb), state)
        diff = bb[:, :, :, None, :] - bb[:, :, None, :, :]
        decay = jnp.exp(jnp.where(causal, diff, -jnp.inf))
        scores = jnp.einsum('bhtk,bhtsk,bhsk->bhts', qb, decay, kb)
        o_intra = jnp.einsum('bhts,bhsv->bhtv', scores, vb)
        b_last = bb[:, :, -1:, :]
        k_dec = kb * jnp.exp(b_last - bb)
        new_state = state * jnp.exp(b_last[:, :, 0, :, None]) + jnp.einsum('bhsk,bhsv->bhkv', k_dec, vb)
        return new_state, o_inter + o_intra

    state0 = jnp.zeros((B, HG_HEADS, HG_KDIM, HG_VDIM), f32)
    _, o = lax.scan(step, state0, (qc, kc, vc, bc))
    o = o.transpose(1, 0, 3, 2, 4).reshape(B, S, HG_HEADS, HG_VDIM)
    gate = jax.nn.silu(g.astype(f32)).reshape(B, S, HG_HEADS, HG_VDIM)
    o = rms_norm(o, out_norm) * gate
    return o.reshape(B, S, D_MODEL).astype(h.dtype) @ w_out


def gmlp_mix(h, w_in, b_in, ln_g, ln_b, w_s, b_s, w_out):
    B, S, _ = h.shape
    z = jax.nn.gelu(h @ w_in + b_in)
    u, v = jnp.split(z, 2, axis=-1)
    v = layer_norm(v, ln_g, ln_b)
    nc = S // GM_CHUNK
    vc = v.reshape(B, nc, GM_CHUNK, GM_GROUPS, GM_GDIM)
    ws = w_s * jnp.tril(jnp.ones((GM_CHUNK, GM_CHUNK), w_s.dtype))[None]
    vm = jnp.einsum('gts,bnsgc->bntgc', ws, vc) + b_s.T[None, None, :, :, None]
    return (u * vm.reshape(B, S, GM_HALF)) @ w_out


def sublayer(x, fn, pre_g, post_g, shift, scale, gate, res_w):
    h = rms_norm(x, pre_g) * (1.0 + scale) + shift
    return x + res_w * gate * rms_norm(fn(h), post_g)


def setup_inputs(seed: int = 0) -> dict:
    key = jax.random.key(seed)
    ks = jax.random.split(key, 24)
    f32 = jnp.float32

    def nrm(k, shape, s):
        return s * jax.random.normal(k, shape, f32)

    return {
        "x": nrm(ks[0], (BATCH, SEQ, D_MODEL), 1.0),
        "c": nrm(ks[1], (BATCH, D_MODEL), 1.0),
        "ada_w": nrm(ks[2], (DEPTH, D_MODEL, 3 * N_SUB * D_MODEL), 0.5 * D_MODEL ** -0.5),
        "ada_b": nrm(ks[3], (DEPTH, 3 * N_SUB * D_MODEL), 0.02),
        "norm_pre": 1.0 + nrm(ks[4], (DEPTH, N_SUB, D_MODEL), 0.02),
        "norm_post": 1.0 + nrm(ks[5], (DEPTH, N_SUB, D_MODEL), 0.02),
        "ffn_w_in": nrm(ks[6], (DEPTH, 2, D_MODEL, 2 * D_FF), D_MODEL ** -0.5),
        "ffn_w_out": nrm(ks[7], (DEPTH, 2, D_FF, D_MODEL), D_FF ** -0.5),
        "hg_w_in": nrm(ks[8], (N_HGRN, D_MODEL, 2 * HG_F + 2 * D_MODEL), D_MODEL ** -0.5),
        "hg_w_out": nrm(ks[9], (N_HGRN, D_MODEL, D_MODEL), D_MODEL ** -0.5),
        "hg_out_norm": 1.0 + nrm(ks[10], (N_HGRN, HG_VDIM), 0.02),
        "hg_lb": nrm(ks[11], (DEPTH + 1, HG_F), 0.1),
        "gm_w_in": nrm(ks[12], (N_GMLP, D_MODEL, GM_FFN), D_MODEL ** -0.5),
        "gm_b_in": nrm(ks[13], (N_GMLP, GM_FFN), 0.02),
        "gm_ln_g": 1.0 + nrm(ks[14], (N_GMLP, GM_HALF), 0.02),
        "gm_ln_b": nrm(ks[15], (N_GMLP, GM_HALF), 0.02),
        "gm_w_s": nrm(ks[16], (N_GMLP, GM_GROUPS, GM_CHUNK, GM_CHUNK), GM_CHUNK ** -0.5),
        "gm_b_s": 1.0 + nrm(ks[17], (N_GMLP, GM_GROUPS, GM_CHUNK), 0.02),
        "gm_w_out": nrm(ks[18], (N_GMLP, GM_HALF, D_MODEL), GM_HALF ** -0.5),
    }


def reference(x, c, ada_w, ada_b, norm_pre, norm_post, ffn_w_in, ffn_w_out,
              hg_w_in, hg_w_out, hg_out_norm, hg_lb,
              gm_w_in, gm_b_in, gm_ln_g, gm_ln_b, gm_w_s, gm_b_s, gm_w_out):
    B = x.shape[0]
    lb_all = jnp.cumsum(jax.nn.softmax(hg_lb.astype(jnp.float32), axis=0), axis=0)
    cond = jax.nn.silu(c)
    for i in range(DEPTH):
        mod = (cond @ ada_w[i] + ada_b[i]).reshape(B, 3 * N_SUB, D_MODEL)[:, :, None, :]
        j = i // N_MIXERS
        if i % N_MIXERS == 0:
            mixer = functools.partial(hgrn2_mix, w_in=hg_w_in[j], w_out=hg_w_out[j],
                                      out_norm=hg_out_norm[j], lb=lb_all[i])
        else:
            mixer = functools.partial(gmlp_mix, w_in=gm_w_in[j], b_in=gm_b_in[j], ln_g=gm_ln_g[j],
                                      ln_b=gm_ln_b[j], w_s=gm_w_s[j], b_s=gm_b_s[j], w_out=gm_w_out[j])
        fns = (functools.partial(swiglu, w_in=ffn_w_in[i, 0], w_out=ffn_w_out[i, 0]),
               mixer,
               functools.partial(swiglu, w_in=ffn_w_in[i, 1], w_out=ffn_w_out[i, 1]))
        res_ws = (0.5, 1.0, 0.5)
        for s in range(N_SUB):
            x = sublayer(x, fns[s], norm_pre[i, s], norm_post[i, s],
                         mod[:, 3 * s], mod[:, 3 * s + 1], mod[:, 3 * s + 2], res_ws[s])
    return x
```

```python
import contextlib
import numpy as np
import concourse.bass as bass
import concourse.mybir as mybir
from concourse.bass_utils import run_bass_kernel_spmd

F32 = mybir.dt.float32
BF16 = mybir.dt.bfloat16
AF = mybir.ActivationFunctionType
ALU = mybir.AluOpType

D = 1024
KC = 8
DFF = 2816
FC = 22
T = 512
EPS = 1e-6
GMH = 3072
NCORES = 8
NTOK = 4096
NSLOT = 3
SLOT_ELEMS = 4096
GELU_C = 0.7978845608028654


class Prog:
    ENGS = ("pe", "act", "dve", "pool", "sp")

    def __init__(self):
        self.q = {e: [] for e in self.ENGS}
        self.cnt = {}
        self.res = {}
        self.known = {e: {} for e in self.ENGS}
        self.semkeys = []
        self.dry = False
        self.nops = 0

    def _tok(self, semkey, inc):
        if semkey not in self.cnt:
            self.cnt[semkey] = 0
            self.semkeys.append(semkey)
        self.cnt[semkey] += inc
        return (semkey, self.cnt[semkey])

    def op(self, eng, fn, r=(), w=(), dma=None):
        if self.dry:
            return
        deps = []
        for k in r:
            st = self.res.get(k)
            if st and st[0] is not None:
                deps.append(st[0])
        for k in w:
            st = self.res.get(k)
            if st:
                if st[0] is not None:
                    deps.append(st[0])
                deps.extend(st[1])
        waits = []
        kn = self.known[eng]
        best = {}
        for (sk, v) in deps:
            if eng == "pe" and sk == ("E", "pe"):
                continue
            if kn.get(sk, 0) >= v:
                continue
            if best.get(sk, 0) < v:
                best[sk] = v
        for sk, v in best.items():
            kn[sk] = v
            waits.append((sk, v))
        if fn is None:
            self.q[eng].append((waits, None, None))
            return
        if dma is not None:
            tok = self._tok(("D", dma), 16)
            inc = (tok[0], 16)
        else:
            tok = self._tok(("E", eng), 1)
            inc = (tok[0], 1)
        self.q[eng].append((waits, fn, inc))
        self.nops += 1
        for k in r:
            st = self.res.setdefault(k, [None, []])
            st[1].append(tok)
        for k in w:
            self.res[k] = [tok, []]

    def emit(self, nc):
        with contextlib.ExitStack() as es:
            sems = {}
            for i, sk in enumerate(self.semkeys):
                sems[sk] = es.enter_context(nc.semaphore("s%d" % i))
            block = es.enter_context(nc.Block())

            def replay(name, e):
                for (waits, fn, inc) in self.q[name]:
                    for (sk, v) in waits:
                        e.wait_ge(sems[sk], v)
                    if fn is not None:
                        inst = fn(e)
                        inst.then_inc(sems[inc[0]], inc[1])

            @block.tensor
            def _(e):
                replay("pe", e)

            @block.scalar
            def _(e):
                replay("act", e)

            @block.vector
            def _(e):
                replay("dve", e)

            @block.gpsimd
            def _(e):
                replay("pool", e)

            @block.sync
            def _(e):
                replay("sp", e)


def wspec(kind):
    if kind == "ffn_in":
        return 8, 512, [[(b * 256, 256), (DFF + b * 256, 256)] for b in range(11)]
    if kind == "ffn_out":
        return 22, 128, [[(b * 128, 128)] for b in range(8)]
    if kind == "hg_in":
        return 8, 512, [[(b * 512, 512)] for b in range(8)]
    if kind == "hg_out":
        return 8, 512, [[(b * 512, 512)] for b in range(2)]
    if kind == "gm_in":
        return 8, 512, [[(b * 512, 512)] for b in range(12)]
    if kind == "gm_out":
        return 24, 128, [[(b * 128, 128)] for b in range(8)]
    raise KeyError(kind)


class Builder:
    def __init__(self, cfg):
        self.cfg = cfg
        self.nt = cfg.get("ntile", 8)
        self.ntok = self.nt * T
        self.mode = cfg.get("mode", "F")
        self.nc = bass.Bass("TRN2", target_bir_lowering=False)
        self.P = Prog()
        self.es = contextlib.ExitStack()
        self.wreq = []
        self.wpos = 0
        self.psrr = {}
        self.rr = {}
        self.scratch = {}

    def dram_in(self, name, shape, dt=F32):
        return self.nc.dram_tensor(name, list(shape), dt, kind="ExternalInput").ap()

    def dram_out(self, name, shape, dt=F32):
        return self.nc.dram_tensor(name, list(shape), dt, kind="ExternalOutput").ap()

    def dram_tmp(self, name, shape, dt):
        return self.nc.dram_tensor(name, list(shape), dt).ap()

    def sb(self, name, shape, dt):
        return self.es.enter_context(self.nc.sbuf_tensor(name, list(shape), dt))

    def rot(self, name, n):
        i = self.rr.get(name, 0)
        self.rr[name] = (i + 1) % n
        return i

    def psum(self, pool):
        lo, n = {"mm": (0, 4), "st": (4, 2), "ms": (6, 2)}[pool]
        i = self.psrr.get(pool, 0)
        self.psrr[pool] = (i + 1) % n
        return lo + i

    def wget(self, wname, blk):
        P = self.P
        if P.dry:
            self.wreq.append((wname, blk))
            return 0
        pos = self.wpos
        assert self.wreq[pos] == (wname, blk), (self.wreq[pos], wname, blk)
        if pos == 0:
            for j in range(min(NSLOT - 1, len(self.wreq))):
                self._wload(j)
        nxt = pos + NSLOT - 1
        if nxt < len(self.wreq):
            self._wload(nxt)
        self.wpos += 1
        return pos % NSLOT

    def _wload(self, j):
        wname, blk = self.wreq[j]
        slot = j % NSLOT
        scr = self.scratch[wname]
        n = scr.shape[2]
        self.P.op("sp", lambda e, d=self.wslots[:, slot, 0:n], s=scr[blk]: e.dma_start(out=d, in_=s),
                  r=[("scr", wname, blk)], w=[("wslot", slot)], dma=("wslot", slot))

    def wview(self, slot, nk, gw):
        return self.wslots[:, slot, 0:nk * gw].rearrange("p (k n) -> p k n", n=gw)

    def stage_keys(self, si):
        if si == 0:
            return [("ysb", k) for k in range(KC)]
        return [("tmpF", i) for i in range(6)] + [("rstd", 0), ("rstd", 1)]

    def cvt_keys(self, ci):
        return [("act", ci * 8 + k) for k in range(8)]

    def convert_weight(self, wname, kind, src):
        P = self.P
        nk, gw, blocks = wspec(kind)
        if wname not in self.scratch:
            self.scratch[wname] = self.dram_tmp("scr_" + wname, [len(blocks), 128, nk * gw], BF16)
        scr = self.scratch[wname]
        if P.dry:
            return
        srcv = src.rearrange("(k p) n -> p k n", p=128)
        for bi, segs in enumerate(blocks):
            si = self.rot("stg", 2)
            stg = self.stage[:, si, 0:nk * gw].rearrange("p (k n) -> p k n", n=gw)
            off = 0
            for gi, (c0, wdt) in enumerate(segs):
                P.op("sp", lambda e, d=stg[:, :, off:off + wdt], s=srcv[:, :, c0:c0 + wdt]: e.dma_start(out=d, in_=s),
                     w=self.stage_keys(si) if gi == 0 else [("stgB", si)], r=[] if gi == 0 else self.stage_keys(si)[:1],
                     dma=("stg", si, gi))
                off += wdt
            ci = self.rot("cvt", 2)
            ceng = ("dve", "pool")[bi % 2]
            cv = self.cvt[:, ci, 0:nk * gw]
            P.op(ceng, lambda e, d=cv, s=self.stage[:, si, 0:nk * gw]: e.tensor_copy(out=d, in_=s),
                 r=self.stage_keys(si) + [("stgB", si)], w=self.cvt_keys(ci))
            P.op("act", lambda e, d=scr[bi], s=cv: e.dma_start(out=d, in_=s),
                 r=self.cvt_keys(ci), w=[("scr", wname, bi)], dma=("cvtout", ci))

    def build(self):
        nc = self.nc
        ntok = self.ntok
        mode = self.mode
        self.xin = self.dram_in("xT", [D, ntok])
        self.vecs = self.dram_in("vecs", [128, NV])
        self.cst = self.dram_in("cst", [128, NCST])
        self.ada_w = self.dram_in("ada_w", [2, D, 9 * D])
        self.ffn_w_in = self.dram_in("ffn_w_in", [2, 2, D, 2 * DFF])
        self.ffn_w_out = self.dram_in("ffn_w_out", [2, 2, DFF, D])
        self.hg_w_in = self.dram_in("hg_w_in", [1, D, 4 * D])
        self.hg_w_out = self.dram_in("hg_w_out", [1, D, D])
        self.gm_w_in = self.dram_in("gm_w_in", [1, D, 2 * GMH])
        self.gm_w_out = self.dram_in("gm_w_out", [1, GMH, D])
        self.gm_rows = self.dram_in("gm_rows", [3, GMH + 2 * GMH + 1024])
        self.gm_wsT = self.dram_in("gm_wsT", [128, 8, 128])
        if mode in ("B", "T"):
            self.Sall = self.dram_in("Sall", [NCORES, 128, 1024])
            self.Pall = self.dram_in("Pall", [NCORES, 128, 8])
        if mode == "A":
            self.yout = self.dram_out("x1T", [D, ntok])
            self.Sloc_o = self.dram_out("Sloc", [128, 1024])
            self.Ploc_o = self.dram_out("Ploc", [128, 8])
        else:
            self.yout = self.dram_out("yT", [D, ntok])
        if self.cfg.get("dbg"):
            self.dbg = self.dram_out("dbg", [128, KC * T], BF16)
        if mode == "F":
            self.x1scr = self.dram_tmp("x1scr", [D, ntok], F32)
            self.Sloc_d = self.dram_tmp("Sloc_d", [128, 1024 + 8], F32)
            self.Sall_d = self.dram_tmp("Sall_d", [NCORES * 128, 1024 + 8], F32)
        self.xT = self.sb("xT_sb", [128, KC, T], F32)
        self.hT = self.sb("hT_sb", [128, KC, T], BF16)
        self.bigF = self.sb("bigF", [128, 8192], F32)
        self.ysb = self.bigF[:, 0:4096].rearrange("p (k n) -> p k n", n=T)
        self.tmpF = self.bigF[:, 4096:7168].rearrange("p (k n) -> p k n", n=T)
        self.rstd = self.bigF[:, 7168:8192].rearrange("p (k n) -> p k n", n=T)
        self.stage = self.bigF[:, :].rearrange("p (k n) -> p k n", n=4096)
        self.sq = self.sb("sq", [128, 2, T], BF16)
        self.actb = self.sb("actb", [128, 24 * T], BF16)
        self.act = self.actb[:, :].rearrange("p (k n) -> p k n", n=T)
        self.cvt = self.actb[:, 0:8192].rearrange("p (k n) -> p k n", n=4096)
        self.bufB = self.sb("bufB", [128, 24 * T], BF16)
        self.wslots = self.sb("wslots", [128, NSLOT, SLOT_ELEMS], BF16)
        self.vec_sb = self.sb("vec_sb", [128, NV], F32)
        self.cst_sb = self.sb("cst_sb", [128, NCST], F32)
        self.cst_bf = self.sb("cst_bf", [128, NCST], BF16)
        self.mod = self.sb("mod_sb", [128, 2 * 72], F32)
        self.coef = self.sb("coef_sb", [128, 2 * 3 * 3 * KC], F32)
        self.small = self.sb("small_sb", [128, 256], F32)
        self.S32 = self.sb("S32", [128, 8, 128], F32)
        self.hist = self.sb("hist", [128, 2, 8, 128], F32)
        self.Sbf = self.sb("Sbf", [128, 2, 8, 128], BF16)
        self.scm = self.sb("scm", [128, 2, T], BF16)
        self.kdB = self.sb("kdB", [128, 4096], BF16)
        self.ebl = self.sb("ebl", [128, 2, 8, 8], F32)
        self.Cc = self.sb("Cc", [128, 24, 128], F32)
        self.wsTm = self.sb("wsTm", [128, 8, 128], BF16)
        self.brow = self.sb("brow", [128, 2048], BF16)
        self.stats = self.sb("stats", [128, 4, 6, 6], F32)
        self.pst = [self.es.enter_context(nc.psum_tensor("ps%d" % i, [128, T], F32)) for i in range(8)]

        for dry in (True, False):
            self.P.dry = dry
            self.rr = {}
            self.psrr = {}
            self.program()
        self.P.emit(nc)
        self.es.close()
        return nc

    def program(self):
        P = self.P
        mode = self.mode
        P.op("sp", lambda e: e.dma_start(out=self.vec_sb[:], in_=self.vecs), w=["vec"], dma="vec")
        P.op("sp", lambda e: e.dma_start(out=self.cst_sb[:], in_=self.cst), w=["cst"], dma="cst")
        P.op("dve", lambda e: e.tensor_copy(out=self.cst_bf[:], in_=self.cst_sb[:]), r=["cst"], w=["cstbf"])
        self.prologue_mod()
        if mode == "T":
            subl = self.cfg["sublayers"]
            need = set(subl)
        elif mode == "A":
            need = {(0, 0), "hgA"}
        elif mode == "B":
            need = {(0, 1), (0, 2), (1, 0), (1, 1), (1, 2)}
        else:
            need = {(0, 0), "hgA", (0, 1), (0, 2), (1, 0), (1, 1), (1, 2)}
        if (0, 0) in need:
            self.conv_ffn(0, 0)
        if "hgA" in need or (0, 1) in need:
            self.convert_weight("hg_in", "hg_in", self.hg_w_in[0])
        if (0, 1) in need:
            self.convert_weight("hg_out", "hg_out", self.hg_w_out[0])
        if (0, 2) in need:
            self.conv_ffn(0, 1)
        if (1, 0) in need:
            self.conv_ffn(1, 0)
        if (1, 1) in need:
            self.convert_weight("gm_in", "gm_in", self.gm_w_in[0])
            self.convert_weight("gm_out", "gm_out", self.gm_w_out[0])
        if (1, 2) in need:
            self.conv_ffn(1, 1)
        if (1, 1) in need:
            self.prologue_gmlp()
        if (0, 1) in need or "hgA" in need:
            self.prologue_hgrn()

        if mode == "T":
            self.init_state_from_all()
            for t in range(self.nt):
                self.load_x(t, self.xin)
                for (li, s) in subl:
                    self.sublayer(li, s, t)
                self.store_x(t, self.yout)
        elif mode == "A":
            self.zero_state()
            for t in range(self.nt):
                self.load_x(t, self.xin)
                self.sublayer(0, 0, t)
                self.store_x(t, self.yout)
                self.prenorm(0, 1)
                self.hgrn(t, state_only=True)
            P.op("sp", lambda e: e.dma_start(out=self.Sloc_o, in_=self.S32[:].rearrange("p h v -> p (h v)")),
                 r=["S32"], w=["Sloc_o"], dma="so1")
            P.op("sp", lambda e: e.dma_start(out=self.Ploc_o, in_=self.small[:, SM_PTOT:SM_PTOT + 8]),
                 r=["ptot"], w=["Ploc_o"], dma="so2")
            P.op("sp", None, r=["Sloc_o", "Ploc_o"])
        elif mode == "B":
            self.init_state_from_all()
            for t in range(self.nt):
                self.load_x(t, self.xin)
                for (li, s) in [(0, 1), (0, 2), (1, 0), (1, 1), (1, 2)]:
                    self.sublayer(li, s, t)
                self.store_x(t, self.yout)
        else:
            self.zero_state()
            for t in range(self.nt):
                self.load_x(t, self.xin)
                self.sublayer(0, 0, t)
                self.store_x(t, self.x1scr, key="x1scr")
                self.prenorm(0, 1)
                self.hgrn(t, state_only=True)
            self.exchange()
            self.init_state_from_all()
            for t in range(self.nt):
                self.load_x(t, self.x1scr, key="x1scr")
                for (li, s) in [(0, 1), (0, 2), (1, 0), (1, 1), (1, 2)]:
                    self.sublayer(li, s, t)
                self.store_x(t, self.yout)
        P.op("sp", None, r=[("yout", t) for t in range(self.nt)])

    def conv_ffn(self, li, j):
        self.convert_weight("ffn_in_%d_%d" % (li, j), "ffn_in", self.ffn_w_in[li, j])
        self.convert_weight("ffn_out_%d_%d" % (li, j), "ffn_out", self.ffn_w_out[li, j])

    def prologue_mod(self):
        P = self.P
        vs = self.vec_sb
        tmp = self.small[:, SM_TMP:SM_TMP + KC]
        cond = self.small[:, SM_COND:SM_COND + KC]
        P.op("act", lambda e: e.activation(out=tmp, in_=vs[:, V_C:V_C + KC], func=AF.Tanh, scale=0.5),
             r=["vec"], w=["smtmp"])
        P.op("dve", lambda e: e.tensor_scalar(out=tmp, in0=tmp, scalar1=0.5, scalar2=0.5, op0=ALU.mult, op1=ALU.add),
             r=["smtmp"], w=["smtmp"])
        P.op("dve", lambda e: e.tensor_tensor(out=cond, in0=tmp, in1=vs[:, V_C:V_C + KC], op=ALU.mult),
             r=["smtmp", "vec"], w=["cond"])
        for li in range(2):
            bank = self.psum("st")
            pst = self.pst[bank]
            first = True
            for blk in range(18):
                si = self.rot("stg", 2)
                stg = self.stage[:, si, :].rearrange("p (k n) -> p k n", n=512)
                src = self.ada_w[li].rearrange("(k p) n -> p k n", p=128)[:, :, blk * 512:(blk + 1) * 512]
                P.op("sp", lambda e, d=stg, s=src: e.dma_start(out=d, in_=s), w=self.stage_keys(si) + [("stgB", si)],
                     dma=("stg", si, 0))

                def mm(e, stg=stg, blk=blk, pst=pst, first=first):
                    inst = None
                    for m in range(4):
                        col = blk * 4 + m
                        for k in range(KC):
                            inst = e.matmul(pst[:, col:col + 1], lhsT=stg[:, k, m * 128:(m + 1) * 128],
                                            rhs=cond[:, k:k + 1],
                                            start=(first and m == 0 and k == 0), stop=(k == KC - 1),
                                            skip_group_check=True)
                    return inst
                P.op("pe", mm, r=self.stage_keys(si) + ["cond"], w=[("ps", bank)])
                first = False
            P.op("dve", lambda e, li=li, pst=pst: e.tensor_tensor(
                out=self.mod[:, li * 72:(li + 1) * 72], in0=pst[:, 0:72],
                in1=vs[:, V_ADAB + li * 72:V_ADAB + (li + 1) * 72], op=ALU.add),
                r=[("ps", bank), "vec"], w=[("mod", li)])
            for s in range(3):
                base = (li * 3 + s) * 3 * KC
                sh = self.mod[:, li * 72 + (3 * s) * KC: li * 72 + (3 * s + 1) * KC]
                sc = self.mod[:, li * 72 + (3 * s + 1) * KC: li * 72 + (3 * s + 2) * KC]
                gt = self.mod[:, li * 72 + (3 * s + 2) * KC: li * 72 + (3 * s + 3) * KC]
                pre = vs[:, V_PRE + (li * 3 + s) * KC: V_PRE + (li * 3 + s + 1) * KC]
                post = vs[:, V_POST + (li * 3 + s) * KC: V_POST + (li * 3 + s + 1) * KC]
                cA = self.coef[:, base:base + KC]
                cB = self.coef[:, base + KC:base + 2 * KC]
                cG = self.coef[:, base + 2 * KC:base + 3 * KC]
                rw = 1.0 if s == 1 else 0.5
                P.op("dve", lambda e, cA=cA, sc=sc, pre=pre: e.scalar_tensor_tensor(
                    out=cA, in0=sc, scalar=1.0, in1=pre, op0=ALU.add, op1=ALU.mult),
                    r=[("mod", li), "vec"], w=[("coef", li, s)])
                P.op("dve", lambda e, cB=cB, sh=sh: e.tensor_copy(out=cB, in_=sh),
                     r=[("mod", li)], w=[("coef", li, s)])
                P.op("dve", lambda e, cG=cG, gt=gt, post=post, rw=rw: e.scalar_tensor_tensor(
                    out=cG, in0=gt, scalar=rw, in1=post, op0=ALU.mult, op1=ALU.mult),
                    r=[("mod", li), "vec"], w=[("coef", li, s)])

    def load_x(self, t, src, key=None):
        sv = src.rearrange("(k p) n -> p k n", p=128)[:, :, t * T:(t + 1) * T]
        self.P.op("sp", lambda e: e.dma_start(out=self.xT[:], in_=sv), r=[(key, t)] if key else [],
                  w=[("x", k) for k in range(KC)], dma="xload")

    def store_x(self, t, dst, key="yout"):
        dv = dst.rearrange("(k p) n -> p k n", p=128)[:, :, t * T:(t + 1) * T]
        self.P.op("sp", lambda e: e.dma_start(out=dv, in_=self.xT[:]), r=[("x", k) for k in range(KC)],
                  w=[(key, t)], dma="xstore")

    def rstd_from_ss(self, bank, nfeat, eps=EPS):
        P = self.P
        ri = self.rot("rstd", 2)
        rs = self.rstd[:, ri, :]
        P.op("dve", lambda e: e.tensor_scalar(out=rs, in0=self.pst[bank][:], scalar1=1.0 / nfeat, scalar2=eps,
                                              op0=ALU.mult, op1=ALU.add),
             r=[("ps", bank)], w=[("rstd", ri)])
        P.op("act", lambda e: e.activation(out=rs, in_=rs, func=AF.Ln), r=[("rstd", ri)], w=[("rstd", ri)])
        P.op("act", lambda e: e.activation(out=rs, in_=rs, func=AF.Exp, scale=-0.5), r=[("rstd", ri)], w=[("rstd", ri)])
        return ri

    def sumsq(self, srcs, rkeys):
        P = self.P
        bank = self.psum("st")
        ones = self.cst_bf[:, C_ONES:C_ONES + 128]
        n = len(srcs)
        for i, (s, rk) in enumerate(zip(srcs, rkeys)):
            qi = self.rot("sq", 2)
            sqv = self.sq[:, qi, :]
            P.op("act", lambda e, s=s, sqv=sqv: e.activation(out=sqv, in_=s, func=AF.Square), r=rk, w=[("sq", qi)])
            P.op("pe", lambda e, sqv=sqv, i=i: e.matmul(self.pst[bank][:], lhsT=ones, rhs=sqv, start=(i == 0),
                                                        stop=(i == n - 1)),
                 r=[("sq", qi), "cstbf"], w=[("ps", bank)])
        return bank

    def prenorm(self, li, s):
        P = self.P
        base = (li * 3 + s) * 3 * KC
        bank = self.sumsq([self.xT[:, k, :] for k in range(KC)], [[("x", k)] for k in range(KC)])
        ri = self.rstd_from_ss(bank, D)
        rs = self.rstd[:, ri, :]
        for k in range(KC):
            ti = self.rot("tmpF", 6)
            tv = self.tmpF[:, ti, :]
            eng = ("dve", "pool")[k % 2]
            P.op(eng, lambda e, tv=tv, k=k: e.tensor_tensor(out=tv, in0=self.xT[:, k, :], in1=rs, op=ALU.mult),
                 r=[("x", k), ("rstd", ri)], w=[("tmpF", ti)])
            P.op("act", lambda e, tv=tv, k=k: e.activation(
                out=self.hT[:, k, :], in_=tv, func=AF.Identity,
                scale=self.coef[:, base + k:base + k + 1], bias=self.coef[:, base + KC + k:base + KC + k + 1]),
                r=[("tmpF", ti), ("coef", li, s)], w=[("h", k)])

    def postnorm_residual(self, li, s):
        P = self.P
        base = (li * 3 + s) * 3 * KC
        bank = self.sumsq([self.ysb[:, k, :] for k in range(KC)], [[("ysb", k)] for k in range(KC)])
        ri = self.rstd_from_ss(bank, D)
        rs = self.rstd[:, ri, :]
        for k in range(KC):
            ti = self.rot("tmpF", 6)
            tv = self.tmpF[:, ti, :]
            eng = ("pool", "dve")[k % 2]
            P.op(eng, lambda e, tv=tv, k=k: e.tensor_tensor(out=tv, in0=self.ysb[:, k, :], in1=rs, op=ALU.mult),
                 r=[("ysb", k), ("rstd", ri)], w=[("tmpF", ti)])
            P.op("dve", lambda e, tv=tv, k=k: e.scalar_tensor_tensor(
                out=self.xT[:, k, :], in0=tv, scalar=self.coef[:, base + 2 * KC + k:base + 2 * KC + k + 1],
                in1=self.xT[:, k, :], op0=ALU.mult, op1=ALU.add),
                r=[("tmpF", ti), ("coef", li, s), ("x", k)], w=[("x", k)])

    def sublayer(self, li, s, t):
        self.prenorm(li, s)
        if s in (0, 2):
            self.ffn(li, s // 2)
        elif li == 0:
            self.hgrn(t, state_only=False)
        else:
            self.gmlp()
        self.postnorm_residual(li, s)

    def ffn(self, li, j):
        P = self.P
        win = "ffn_in_%d_%d" % (li, j)
        wout = "ffn_out_%d_%d" % (li, j)
        for blk in range(11):
            slot = self.wget(win, blk)
            if P.dry:
                continue
            wv = self.wview(slot, 8, 512)
            banks = [self.psum("mm") for _ in range(4)]
            for m in range(4):
                def mm(e, m=m, wv=wv, bank=banks[m]):
                    inst = None
                    for k in range(KC):
                        inst = e.matmul(self.pst[bank][:], lhsT=wv[:, k, m * 128:(m + 1) * 128], rhs=self.hT[:, k, :],
                                        start=(k == 0), stop=(k == KC - 1))
                    return inst
                P.op("pe", mm, r=[("wslot", slot)] + [("h", k) for k in range(KC)], w=[("ps", banks[m])])
            for c in range(2):
                ch = blk * 2 + c
                pa = self.pst[banks[c]]
                pb = self.pst[banks[2 + c]]
                ti = self.rot("tmpF", 6)
                tv = self.tmpF[:, ti, :]
                P.op("act", lambda e, tv=tv, pa=pa: e.activation(out=tv, in_=pa[:], func=AF.Tanh, scale=0.5),
                     r=[("ps", banks[c])], w=[("tmpF", ti)])
                P.op("dve", lambda e, tv=tv, pa=pa: e.scalar_tensor_tensor(
                    out=tv, in0=tv, scalar=1.0, in1=pa[:], op0=ALU.add, op1=ALU.mult),
                    r=[("tmpF", ti), ("ps", banks[c])], w=[("tmpF", ti)])
                P.op("dve", lambda e, tv=tv, pb=pb, ch=ch: e.scalar_tensor_tensor(
                    out=self.act[:, ch, :], in0=tv, scalar=0.5, in1=pb[:], op0=ALU.mult, op1=ALU.mult),
                    r=[("tmpF", ti), ("ps", banks[2 + c])], w=[("act", ch)])
        self.out_proj(wout, FC, 1, [("act", k) for k in range(FC)], lambda k: self.act[:, k, :])

    def out_proj(self, wname, nk, mper, rkeys, rhs_of):
        P = self.P
        for blk in range(8 // mper):
            slot = self.wget(wname, blk)
            if P.dry:
                continue
            wv = self.wview(slot, nk, 128 * mper)
            for mi in range(mper):
                m = blk * mper + mi
                bank = self.psum("mm")

                def mm2(e, wv=wv, bank=bank, mi=mi):
                    inst = None
                    for k in range(nk):
                        inst = e.matmul(self.pst[bank][:], lhsT=wv[:, k, mi * 128:(mi + 1) * 128], rhs=rhs_of(k),
                                        start=(k == 0), stop=(k == nk - 1))
                    return inst
                P.op("pe", mm2, r=[("wslot", slot)] + rkeys, w=[("ps", bank)])
                P.op("act", lambda e, m=m, bank=bank: e.activation(out=self.ysb[:, m, :], in_=self.pst[bank][:],
                                                                   func=AF.Copy),
                     r=[("ps", bank)], w=[("ysb", m)])

    def gelu2(self, bank, out_ap, wkeys, extra_dve=None):
        P = self.P
        ps = self.pst[bank]
        ti = self.rot("tmpF", 6)
        tv = self.tmpF[:, ti, :]
        P.op("act", lambda e: e.activation(out=tv, in_=ps[:], func=AF.Square, scale=0.044715 ** 0.5),
             r=[("ps", bank)], w=[("tmpF", ti)])
        P.op("dve", lambda e: e.scalar_tensor_tensor(out=tv, in0=tv, scalar=1.0, in1=ps[:], op0=ALU.add, op1=ALU.mult),
             r=[("tmpF", ti), ("ps", bank)], w=[("tmpF", ti)])
        P.op("act", lambda e: e.activation(out=tv, in_=tv, func=AF.Tanh, scale=GELU_C),
             r=[("tmpF", ti)], w=[("tmpF", ti)])
        P.op("dve", lambda e: e.scalar_tensor_tensor(out=out_ap, in0=tv, scalar=1.0, in1=ps[:], op0=ALU.add, op1=ALU.mult),
             r=[("tmpF", ti), ("ps", bank)], w=wkeys)

    def prologue_gmlp(self):
        P = self.P
        vs = self.vec_sb
        f32v = self.ysb
        keys = [("ysb", k) for k in range(KC)]
        wsT32 = self.bigF[:, 0:1024].rearrange("p (g t) -> p g t", t=128)
        rs32 = self.bigF[:, 1024:2048]
        lnbrow = self.bigF[:, 2048:2048 + 1024]
        bsrow = self.bigF[:, 3072:3072 + 1024]
        bin32 = self.bigF[:, 4096:4096 + 2048]
        tkeys = [("tmpF", i) for i in range(4)]
        P.op("sp", lambda e: e.dma_start(out=wsT32, in_=self.gm_wsT), w=keys, dma="gp0")
        for j in range(3):
            pp = 32 * j
            P.op("sp", lambda e, pp=pp, j=j: e.dma_start(out=lnbrow[pp:pp + 1, :], in_=self.gm_rows[j:j + 1, j * 1024:(j + 1) * 1024]),
                 w=["gp_lnb%d" % j], dma="gp1_%d" % j)
            P.op("sp", lambda e, pp=pp, j=j: e.dma_start(out=bsrow[pp:pp + 1, :], in_=self.gm_rows[j:j + 1, 3 * GMH:3 * GMH + 1024]),
                 w=["gp_bs%d" % j], dma="gp2_%d" % j)
            P.op("sp", lambda e, pp=pp, j=j: e.dma_start(out=bin32[pp:pp + 1, :],
                                                         in_=self.gm_rows[j:j + 1, GMH + j * 2048:GMH + (j + 1) * 2048]),
                 w=["gp_bin%d" % j] + tkeys, dma="gp3_%d" % j)
        mask = self.cst_sb[:, C_TRIL:C_TRIL + 128]
        P.op("dve", lambda e: e.tensor_tensor(out=wsT32, in0=wsT32, in1=mask.unsqueeze(1).to_broadcast([128, 8, 128]),
                                              op=ALU.mult), r=keys + ["cst"], w=keys)
        P.op("dve", lambda e: e.tensor_copy(out=self.wsTm[:], in_=wsT32), r=keys, w=["wsTm"])
        P.op("dve", lambda e: e.tensor_copy(out=self.brow[:], in_=bin32), r=["gp_bin0", "gp_bin1", "gp_bin2"] + tkeys,
             w=["brow"])
        P.op("dve", lambda e: e.tensor_scalar(out=self.small[:, SM_LNG:SM_LNG + 24], in0=vs[:, V_LNG:V_LNG + 24],
                                              scalar1=0.5, scalar2=None, op0=ALU.mult), r=["vec"], w=["lng"])
        ones32 = self.cst_sb[:, C_ONES:C_ONES + 128]
        for hf in range(2):
            bank = self.psum("ms")
            P.op("pe", lambda e, bank=bank, hf=hf: e.matmul(self.pst[bank][:], lhsT=ones32,
                                                          rhs=self.bigF[:, hf * 512:(hf + 1) * 512], start=True, stop=True),
                 r=keys + ["cst"], w=[("ps", bank)])
            P.op("dve", lambda e, bank=bank, hf=hf: e.tensor_copy(out=rs32[:, hf * 512:(hf + 1) * 512], in_=self.pst[bank][:]),
                 r=[("ps", bank)], w=["gp_rs%d" % hf])
        for cb in range(6):
            bank = self.psum("ms")

            def mm(e, bank=bank, cb=cb):
                inst = None
                for i in range(4):
                    ch = cb * 4 + i
                    g = ch // 3
                    j = ch // 8
                    pp = 32 * j
                    cc = (ch % 8) * 128
                    e.matmul(self.pst[bank][:, i * 128:(i + 1) * 128], lhsT=lnbrow[pp:pp + 1, cc:cc + 128],
                             rhs=rs32[pp:pp + 1, g * 128:(g + 1) * 128], start=(i == 0), stop=False,
                             skip_group_check=True)
                    inst = e.matmul(self.pst[bank][:, i * 128:(i + 1) * 128], lhsT=ones32[pp:pp + 1, :],
                                    rhs=bsrow[pp:pp + 1, g * 128:(g + 1) * 128], start=False, stop=True,
                                    skip_group_check=True)
                return inst
            P.op("pe", mm, r=["gp_rs0", "gp_rs1", "cst"] + ["gp_lnb%d" % j for j in range(3)] + ["gp_bs%d" % j for j in range(3)],
                 w=[("ps", bank)])
            P.op("dve", lambda e, bank=bank, cb=cb: e.tensor_scalar(
                out=self.Cc[:, cb * 4:(cb + 1) * 4, :], in0=self.pst[bank][:].rearrange("p (i t) -> p i t", t=128),
                scalar1=0.5, scalar2=None, op0=ALU.mult), r=[("ps", bank)], w=["Cc"])
        P.op("dve", lambda e: e.tensor_copy(out=self.small[:, SM_TMP:SM_TMP + 1], in_=self.small[:, SM_TMP:SM_TMP + 1]),
             r=["gp_rs0", "gp_rs1", "brow", "wsTm", "smtmp"], w=keys + tkeys + ["smtmp"])

    def gmlp(self):
        P = self.P
        vtok = self.bufB[:, :].rearrange("p (b n) -> p b n", n=GMH)
        ones_bf = self.cst_bf[:, C_ONES:C_ONES + 512]
        for vb in range(6):
            slot = self.wget("gm_in", 6 + vb)
            if P.dry:
                continue
            wv = self.wview(slot, 8, 512)
            col = GMH + vb * 512
            pp = 32 * (col // 2048)
            cc = col % 2048
            for tb in range(4):
                bank = self.psum("mm")

                def mm(e, wv=wv, bank=bank, tb=tb, pp=pp, cc=cc):
                    for k in range(KC):
                        e.matmul(self.pst[bank][:], lhsT=self.hT[:, k, tb * 128:(tb + 1) * 128], rhs=wv[:, k, :],
                                 start=(k == 0), stop=False)
                    return e.matmul(self.pst[bank][:], lhsT=ones_bf[pp:pp + 1, 0:128], rhs=self.brow[pp:pp + 1, cc:cc + 512],
                                    start=False, stop=True)
                P.op("pe", mm, r=[("wslot", slot), "brow", "cstbf"] + [("h", k) for k in range(KC)], w=[("ps", bank)])
                dst = vtok[:, tb, vb * 512:(vb + 1) * 512]
                self.gelu2(bank, dst, [("vtok", tb, vb)])
                P.op("dve", lambda e, dst=dst, tb=tb, vb=vb: e.bn_stats(out=self.stats[:, tb, vb, :], in_=dst),
                     r=[("vtok", tb, vb)], w=[("stats", tb, vb)])
        if not P.dry:
            mv = self.small[:, SM_MV:SM_MV + 8].rearrange("p (b c) -> p b c", c=2)
            for tb in range(4):
                P.op("dve", lambda e, tb=tb: e.bn_aggr(out=mv[:, tb, :], in_=self.stats[:, tb, :, :]),
                     r=[("stats", tb, vb) for vb in range(6)], w=[("mv", tb)])
            rv = self.small[:, SM_RV:SM_RV + 4]
            P.op("dve", lambda e: e.tensor_scalar(out=rv, in0=mv[:, :, 1], scalar1=4.0 * EPS, scalar2=None, op0=ALU.add),
                 r=[("mv", tb) for tb in range(4)], w=["rv"])
            P.op("act", lambda e: e.activation(out=rv, in_=rv, func=AF.Ln), r=["rv"], w=["rv"])
            P.op("act", lambda e: e.activation(out=rv, in_=rv, func=AF.Exp, scale=-0.5), r=["rv"], w=["rv"])
            for tb in range(4):
                for hf in range(2):
                    eng = ("dve", "pool")[hf]
                    vv = vtok[:, tb, hf * 1536:(hf + 1) * 1536]
                    P.op("dve", lambda e, vv=vv, tb=tb: e.tensor_scalar(
                        out=vv, in0=vv, scalar1=mv[:, tb, 0:1], scalar2=rv[:, tb:tb + 1], op0=ALU.subtract, op1=ALU.mult),
                        r=[("vtok", tb, vb) for vb in range(6)] + [("mv", tb), "rv"],
                        w=[("vhat", tb, hf)] + [("vtok", tb, hf * 3 + i) for i in range(3)])
        for ub in range(6):
            slot = self.wget("gm_in", ub)
            if P.dry:
                continue
            wv = self.wview(slot, 8, 512)
            for m in range(4):
                ch = ub * 4 + m
                col = ch * 128
                pp = 32 * (col // 2048)
                cc = col % 2048
                bank = self.psum("mm")

                def mm(e, wv=wv, bank=bank, m=m, pp=pp, cc=cc):
                    for k in range(KC):
                        e.matmul(self.pst[bank][:], lhsT=wv[:, k, m * 128:(m + 1) * 128], rhs=self.hT[:, k, :],
                                 start=(k == 0), stop=False)
                    return e.matmul(self.pst[bank][:], lhsT=self.brow[pp:pp + 1, cc:cc + 128], rhs=ones_bf[pp:pp + 1, 0:512],
                                    start=False, stop=True)
                P.op("pe", mm, r=[("wslot", slot), "brow", "cstbf"] + [("h", k) for k in range(KC)], w=[("ps", bank)])
                self.gelu2(bank, self.act[:, ch, :], [("act", ch)])
        if P.dry:
            self.out_proj("gm_out", 24, 1, None, None)
            return
        for ch in range(24):
            g = ch // 3
            bank = self.psum("ms")

            def mm(e, ch=ch, g=g, bank=bank):
                inst = None
                for tb in range(4):
                    inst = e.matmul(self.pst[bank][:, tb * 128:(tb + 1) * 128], lhsT=vtok[:, tb, ch * 128:(ch + 1) * 128],
                                    rhs=self.wsTm[:, g, :], start=(tb == 0), stop=(tb == 3), skip_group_check=True)
                return inst
            hf = (ch * 128) // 1536
            P.op("pe", mm, r=[("vhat", tb, hf) for tb in range(4)] + ["wsTm"] +
                 [("vtok", tb, hf * 3 + i) for i in range(3) for tb in range(4)], w=[("ps", bank)])
            ti = self.rot("tmpF", 6)
            tv = self.tmpF[:, ti, :]
            P.op("dve", lambda e, tv=tv, bank=bank, ch=ch: e.scalar_tensor_tensor(
                out=tv.rearrange("p (b t) -> p b t", t=128), in0=self.pst[bank][:].rearrange("p (b t) -> p b t", t=128),
                scalar=self.small[:, SM_LNG + ch:SM_LNG + ch + 1],
                in1=self.Cc[:, ch, :].unsqueeze(1).to_broadcast([128, 4, 128]), op0=ALU.mult, op1=ALU.add),
                r=[("ps", bank), "lng", "Cc"], w=[("tmpF", ti)])
            P.op("pool", lambda e, tv=tv, ch=ch: e.tensor_tensor(out=self.act[:, ch, :], in0=tv, in1=self.act[:, ch, :],
                                                                 op=ALU.mult),
                 r=[("tmpF", ti), ("act", ch)], w=[("act", ch)])
        self.out_proj("gm_out", 24, 1, [("act", k) for k in range(24)], lambda k: self.act[:, k, :])

    def prologue_hgrn(self):
        P = self.P
        vs = self.vec_sb
        sm = self.small
        ex = sm[:, SM_EX:SM_EX + 24]
        P.op("act", lambda e: e.activation(out=ex, in_=vs[:, V_LB:V_LB + 24], func=AF.Exp), r=["vec"], w=["ex"])
        den = sm[:, SM_DEN:SM_DEN + 8]
        P.op("dve", lambda e: e.tensor_tensor(out=den, in0=ex[:, 0:8], in1=ex[:, 8:16], op=ALU.add), r=["ex"], w=["den"])
        P.op("dve", lambda e: e.tensor_tensor(out=den, in0=den, in1=ex[:, 16:24], op=ALU.add), r=["ex", "den"], w=["den"])
        P.op("dve", lambda e: e.reciprocal(out=den, in_=den), r=["den"], w=["den"])
        lb = sm[:, SM_LB:SM_LB + 8]
        P.op("dve", lambda e: e.tensor_tensor(out=lb, in0=ex[:, 0:8], in1=den, op=ALU.mult), r=["ex", "den"], w=["lb"])
        P.op("dve", lambda e: e.tensor_scalar(out=sm[:, SM_C1:SM_C1 + 8], in0=lb, scalar1=-0.5, scalar2=0.5,
                                              op0=ALU.mult, op1=ALU.add), r=["lb"], w=["lbc"])
        P.op("dve", lambda e: e.tensor_scalar(out=sm[:, SM_C0:SM_C0 + 8], in0=lb, scalar1=0.5, scalar2=0.5,
                                              op0=ALU.mult, op1=ALU.add), r=["lb"], w=["lbc"])
        P.op("dve", lambda e: e.tensor_scalar(out=sm[:, SM_GN:SM_GN + 1], in0=vs[:, V_ON:V_ON + 1], scalar1=0.5,
                                              scalar2=None, op0=ALU.mult), r=["vec"], w=["gn"])
        P.op("dve", lambda e: e.memset(sm[:, SM_PTOT:SM_PTOT + 8], 1.0), w=["ptot"])

    def zero_state(self):
        self.P.op("dve", lambda e: e.memset(self.S32[:], 0.0), w=["S32"])

    def init_state_from_all(self):
        P = self.P
        sm = self.small
        self.zero_state()
        sa = self.bigF[:, 0:1024].rearrange("p (h v) -> p h v", v=128)
        tm = self.bigF[:, 1024:2048].rearrange("p (h v) -> p h v", v=128)
        keys = [("ysb", 0), ("ysb", 1), ("ysb", 2), ("ysb", 3)]
        pm1 = sm[:, SM_PM1:SM_PM1 + 8]
        for j in range(NCORES):
            P.op("sp", lambda e, j=j: e.dma_start(out=self.bigF[:, 0:1024], in_=self.Sall[j]), w=keys[:2], dma="is0")
            P.op("sp", lambda e, j=j: e.dma_start(out=pm1, in_=self.Pall[j]), w=["pm1"], dma="is1")
            P.op("dve", lambda e: e.tensor_scalar(out=pm1, in0=pm1, scalar1=-1.0, scalar2=None, op0=ALU.add),
                 r=["pm1"], w=["pm1"])
            for h in range(8):
                P.op("dve", lambda e, h=h: e.scalar_tensor_tensor(out=tm[:, h, :], in0=self.S32[:, h, :], scalar=pm1[:, h:h + 1],
                                                                  in1=sa[:, h, :], op0=ALU.mult, op1=ALU.add),
                     r=["S32", "pm1"] + keys[:2], w=keys[2:])
            P.op("dve", lambda e, j=j: e.scalar_tensor_tensor(
                out=self.S32[:].rearrange("p h v -> p (h v)"), in0=self.bigF[:, 1024:2048],
                scalar=self.vec_sb[:, V_MASK + j:V_MASK + j + 1], in1=self.S32[:].rearrange("p h v -> p (h v)"),
                op0=ALU.mult, op1=ALU.add), r=keys[2:] + ["vec", "S32"], w=["S32"])

    def exchange(self):
        P = self.P
        P.op("sp", lambda e: e.dma_start(out=self.Sloc_d[:, 0:1024], in_=self.S32[:].rearrange("p h v -> p (h v)")),
             r=["S32"], w=["Sloc_d"], dma="ex0")
        P.op("sp", lambda e: e.dma_start(out=self.Sloc_d[:, 1024:1032], in_=self.small[:, SM_PTOT:SM_PTOT + 8]),
             r=["ptot", "Sloc_d"], w=["Sloc_d2"], dma="ex1")
        P.op("pool", lambda e: e.collective_compute("AllGather", ALU.bypass, [list(range(NCORES))],
                                                    [self.Sloc_d], [self.Sall_d]),
             r=["Sloc_d", "Sloc_d2"], w=["Sall_d"], dma="ex2")
        sv = self.Sall_d.rearrange("(j p) n -> j p n", p=128)
        self.Sall = sv[:, :, 0:1024]
        self.Pall = sv[:, :, 1024:1032]
        P.op("sp", None, r=["Sall_d"])

    def hgrn(self, t, state_only):
        P = self.P
        sm = self.small
        q_t = lambda h: self.act[:, h, :]
        kn_t = lambda h: self.act[:, 8 + h, :]
        kd_t = lambda h: self.act[:, 16 + h, :]
        kdtok = self.bufB[:, 0:4096].rearrange("p (b n) -> p b n", n=1024)
        vtok = self.bufB[:, 4096:8192].rearrange("p (b n) -> p b n", n=1024)
        kdtokB = self.kdB[:, :].rearrange("p (b n) -> p b n", n=1024)
        sgate = self.bufB[:, 8192:12288].rearrange("p (h n) -> p h n", n=T)
        ones32 = self.cst_sb[:, C_ONES:C_ONES + 64]
        rmask = self.cst_sb[:, C_RMASK:C_RMASK + T]
        ident = self.cst_bf[:, C_IDENT:C_IDENT + 128]

        def proj_fm(blk, consume):
            slot = self.wget("hg_in", blk)
            if P.dry:
                return
            wv = self.wview(slot, 8, 512)
            for m in range(4):
                bank = self.psum("mm")

                def mm(e, wv=wv, bank=bank, m=m):
                    inst = None
                    for k in range(KC):
                        inst = e.matmul(self.pst[bank][:], lhsT=wv[:, k, m * 128:(m + 1) * 128], rhs=self.hT[:, k, :],
                                        start=(k == 0), stop=(k == KC - 1))
                    return inst
                P.op("pe", mm, r=[("wslot", slot)] + [("h", k) for k in range(KC)], w=[("ps", bank)])
                consume((blk % 2) * 4 + m, bank)

        def cons_q(h, bank):
            ti = self.rot("tmpF", 6)
            tv = self.tmpF[:, ti, :]
            P.op("act", lambda e: e.activation(out=tv, in_=self.pst[bank][:], func=AF.Tanh, scale=0.5),
                 r=[("ps", bank)], w=[("tmpF", ti)])
            P.op("dve", lambda e: e.scalar_tensor_tensor(out=self.ysb[:, h, :], in0=tv, scalar=1.0, in1=self.pst[bank][:],
                                                         op0=ALU.add, op1=ALU.mult),
                 r=[("tmpF", ti), ("ps", bank)], w=[("ysb", h)])

        def cons_f(h, bank):
            ia, ib, ic, idd = [self.rot("tmpF", 6) for _ in range(4)]
            a, b, c, d = [self.tmpF[:, i, :] for i in (ia, ib, ic, idd)]
            P.op("act", lambda e: e.activation(out=a, in_=self.pst[bank][:], func=AF.Tanh, scale=0.5),
                 r=[("ps", bank)], w=[("tmpF", ia)])
            P.op("dve", lambda e: e.tensor_scalar(out=a, in0=a, scalar1=sm[:, SM_C1 + h:SM_C1 + h + 1],
                                                  scalar2=sm[:, SM_C0 + h:SM_C0 + h + 1], op0=ALU.mult, op1=ALU.add),
                 r=[("tmpF", ia), "lbc"], w=[("tmpF", ia)])
            P.op("act", lambda e: e.activation(out=b, in_=a, func=AF.Ln), r=[("tmpF", ia)], w=[("tmpF", ib)])
            P.op("dve", lambda e: e.tensor_tensor_scan(out=c, data0=rmask, data1=b, initial=0.0, op0=ALU.mult, op1=ALU.add),
                 r=[("tmpF", ib), "cst"], w=[("tmpF", ic)])
            P.op("act", lambda e: e.activation(out=b, in_=c, func=AF.Exp), r=[("tmpF", ic)], w=[("tmpF", ib)])
            P.op("act", lambda e: e.activation(out=d, in_=c, func=AF.Exp, scale=-1.0), r=[("tmpF", ic)], w=[("tmpF", idd)])
            P.op("dve", lambda e: e.scalar_tensor_tensor(out=kn_t(h), in0=a, scalar=-1.0, in1=d, op0=ALU.add, op1=ALU.mult),
                 r=[("tmpF", ia), ("tmpF", idd)], w=[("act", 8 + h)])
            ebv = b.rearrange("p (c s) -> p c s", s=64)[:, :, 63]
            P.op("dve", lambda e: e.tensor_copy(out=self.ebl[:, 0, h, :], in_=ebv), r=[("tmpF", ib)], w=[("ebl", h)])
            P.op("dve", lambda e: e.tensor_scalar(out=self.ebl[:, 1, h, :], in0=ebv, scalar1=-1.0, scalar2=None, op0=ALU.mult),
                 r=[("tmpF", ib)], w=[("nebl", h)])
            for cc in range(8):
                P.op("dve", lambda e, cc=cc: e.tensor_scalar(
                    out=kd_t(h)[:, cc * 64:(cc + 1) * 64], in0=kn_t(h)[:, cc * 64:(cc + 1) * 64],
                    scalar1=self.ebl[:, 1, h, cc:cc + 1], scalar2=None, op0=ALU.mult),
                    r=[("act", 8 + h), ("nebl", h)], w=[("act", 16 + h)])
            if state_only:
                pr = sm[:, SM_PR + h:SM_PR + h + 1]
                P.op("dve", lambda e: e.tensor_reduce(out=pr, in_=self.ebl[:, 0, h, :], axis=mybir.AxisListType.X, op=ALU.mult),
                     r=[("ebl", h)], w=[("pr", h)])
                P.op("dve", lambda e: e.tensor_tensor(out=sm[:, SM_PTOT + h:SM_PTOT + h + 1],
                                                      in0=sm[:, SM_PTOT + h:SM_PTOT + h + 1], in1=pr, op=ALU.mult),
                     r=[("pr", h), "ptot"], w=["ptot"])
            else:
                P.op("dve", lambda e: e.scalar_tensor_tensor(out=q_t(h), in0=self.ysb[:, h, :], scalar=0.5, in1=b,
                                                             op0=ALU.mult, op1=ALU.mult),
                     r=[("ysb", h), ("tmpF", ib)], w=[("act", h)])

        def cons_g(h, bank):
            ti = self.rot("tmpF", 6)
            tv = self.tmpF[:, ti, :]
            P.op("act", lambda e: e.activation(out=tv, in_=self.pst[bank][:], func=AF.Tanh, scale=0.5),
                 r=[("ps", bank)], w=[("tmpF", ti)])
            P.op("dve", lambda e: e.scalar_tensor_tensor(out=sgate[:, h, :], in0=tv, scalar=1.0, in1=self.pst[bank][:],
                                                         op0=ALU.add, op1=ALU.mult),
                 r=[("tmpF", ti), ("ps", bank)], w=[("sgate", h)])

        if not state_only:
            proj_fm(0, cons_q)
            proj_fm(1, cons_q)
        proj_fm(2, cons_f)
        proj_fm(3, cons_f)
        for blk in (4, 5):
            slot = self.wget("hg_in", blk)
            if P.dry:
                continue
            wv = self.wview(slot, 8, 512)
            for tb in range(4):
                bank = self.psum("mm")

                def mm(e, wv=wv, bank=bank, tb=tb):
                    inst = None
                    for k in range(KC):
                        inst = e.matmul(self.pst[bank][:], lhsT=self.hT[:, k, tb * 128:(tb + 1) * 128], rhs=wv[:, k, :],
                                        start=(k == 0), stop=(k == KC - 1))
                    return inst
                P.op("pe", mm, r=[("wslot", slot)] + [("h", k) for k in range(KC)], w=[("ps", bank)])
                P.op("act", lambda e, bank=bank, tb=tb, blk=blk: e.activation(
                    out=vtok[:, tb, (blk - 4) * 512:(blk - 3) * 512], in_=self.pst[bank][:], func=AF.Copy),
                    r=[("ps", bank)], w=[("vtok", tb, blk - 4)])
        if not state_only:
            proj_fm(6, cons_g)
            proj_fm(7, cons_g)
        if P.dry:
            if not state_only:
                self.out_proj("hg_out", 8, 4, None, None)
            return
        for pr in range(4):
            bank = self.psum("ms")
            pbf = self.pst[bank][:].bitcast(BF16)

            def tr(e, pr=pr, pbf=pbf):
                inst = None
                for h in range(8):
                    inst = e.transpose(pbf[:, h * 128:(h + 1) * 128], kd_t(h)[:, pr * 128:(pr + 1) * 128], ident)
                return inst
            P.op("pe", tr, r=[("act", 16 + h) for h in range(8)] + ["cstbf"], w=[("ps", bank)])
            P.op("act", lambda e, pr=pr, pbf=pbf: e.activation(out=kdtok[:, pr, :], in_=pbf, func=AF.Identity,
                                                               scale=self.cst_sb[:, C_HM:C_HM + 1]),
                 r=[("ps", bank), "cst"], w=[("kdtok", pr)])
            P.op("act", lambda e, pr=pr, pbf=pbf: e.activation(out=kdtokB[:, pr, :], in_=pbf, func=AF.Identity,
                                                               scale=self.cst_sb[:, C_HM + 1:C_HM + 2]),
                 r=[("ps", bank), "cst"], w=[("kdtokB", pr)])
        for h in range(8):
            bA = self.psum("mm")
            bB = self.psum("mm")

            def mmU(e, h=h, bA=bA, bB=bB):
                inst = None
                for c in range(8):
                    pr, half = c // 2, c % 2
                    bank = (bA, bB)[half]
                    kk = (kdtok, kdtokB)[half]
                    inst = e.matmul(self.pst[bank][:, (c // 2) * 128:(c // 2 + 1) * 128],
                                    lhsT=kk[:, pr, h * 128:(h + 1) * 128],
                                    rhs=vtok[:, pr, h * 128:(h + 1) * 128],
                                    start=(c < 2), stop=(c >= 6), skip_group_check=True)
                return inst
            hb = h // 4
            P.op("pe", mmU, r=[("kdtok", pr) for pr in range(4)] + [("kdtokB", pr) for pr in range(4)] +
                 [("vtok", tb, hb) for tb in range(4)],
                 w=[("ps", bA), ("ps", bB)])
            hi = self.rot("hist", 2)
            for c in range(8):
                bank = (bA, bB)[c % 2]
                src = self.S32[:, h, :] if c == 0 else self.hist[:, hi, c - 1, :]
                P.op("dve", lambda e, c=c, bank=bank, src=src, hi=hi, h=h: e.scalar_tensor_tensor(
                    out=self.hist[:, hi, c, :], in0=src, scalar=self.ebl[:, 0, h, c:c + 1],
                    in1=self.pst[bank][:, (c // 2) * 128:(c // 2 + 1) * 128], op0=ALU.mult, op1=ALU.add),
                    r=[("ps", bank), ("ebl", h), "S32", ("hist", hi)], w=[("hist", hi)])
            if not state_only:
                P.op("pool", lambda e, hi=hi, h=h: e.tensor_copy(out=self.Sbf[:, hi, 0, :], in_=self.S32[:, h, :]),
                     r=["S32"], w=[("Sbf", hi)])
                P.op("pool", lambda e, hi=hi: e.tensor_copy(out=self.Sbf[:, hi, 1:8, :], in_=self.hist[:, hi, 0:7, :]),
                     r=[("hist", hi)], w=[("Sbf", hi)])
            P.op("pool", lambda e, hi=hi, h=h: e.tensor_copy(out=self.S32[:, h, :], in_=self.hist[:, hi, 7, :]),
                 r=[("hist", hi), ("Sbf", hi)], w=["S32"])
            if state_only:
                continue
            bs = self.psum("ms")

            def mmS(e, h=h, bs=bs):
                inst = None
                for pr in range(4):
                    inst = e.matmul(self.pst[bs][:, pr * 128:(pr + 1) * 128], lhsT=kn_t(h)[:, pr * 128:(pr + 1) * 128],
                                    rhs=q_t(h)[:, pr * 128:(pr + 1) * 128], start=(pr == 0), stop=(pr == 3),
                                    skip_group_check=True)
                return inst
            P.op("pe", mmS, r=[("act", 8 + h), ("act", h)], w=[("ps", bs)])
            si = self.rot("scm", 2)
            P.op("dve", lambda e, bs=bs, si=si: e.tensor_tensor(
                out=self.scm[:, si, :].rearrange("p (b t) -> p b t", t=128),
                in0=self.pst[bs][:].rearrange("p (b t) -> p b t", t=128),
                in1=self.cst_sb[:, C_MNEG:C_MNEG + 128].unsqueeze(1).to_broadcast([128, 4, 128]), op=ALU.mult),
                r=[("ps", bs), "cst"], w=[("scm", si)])
            bo = self.psum("mm")

            def mmO(e, h=h, bo=bo, hi=hi, si=si):
                inst = None
                for c in range(8):
                    e.matmul(self.pst[bo][:, c * 64:(c + 1) * 64], lhsT=self.Sbf[:, hi, c, :],
                             rhs=q_t(h)[:, c * 64:(c + 1) * 64], start=(c == 0), stop=False, skip_group_check=True)
                for pr in range(4):
                    inst = e.matmul(self.pst[bo][:, pr * 128:(pr + 1) * 128], lhsT=vtok[:, pr, h * 128:(h + 1) * 128],
                                    rhs=self.scm[:, si, pr * 128:(pr + 1) * 128], start=False, stop=(pr == 3),
                                    skip_group_check=True)
                return inst
            P.op("pe", mmO, r=[("Sbf", hi), ("act", h), ("scm", si)] + [("vtok", tb, hb) for tb in range(4)],
                 w=[("ps", bo)])
            bss = self.sumsq([self.pst[bo][:]], [[("ps", bo)]])
            ri = self.rstd_from_ss(bss, 128)
            ti = self.rot("tmpF", 6)
            tv = self.tmpF[:, ti, :]
            P.op("dve", lambda e, bo=bo, ri=ri, tv=tv: e.tensor_tensor(out=tv, in0=self.pst[bo][:], in1=self.rstd[:, ri, :],
                                                                       op=ALU.mult),
                 r=[("ps", bo), ("rstd", ri)], w=[("tmpF", ti)])
            P.op("dve", lambda e, tv=tv, h=h: e.scalar_tensor_tensor(
                out=self.hT[:, h, :], in0=tv, scalar=sm[:, SM_GN:SM_GN + 1], in1=sgate[:, h, :], op0=ALU.mult, op1=ALU.mult),
                r=[("tmpF", ti), "gn", ("sgate", h)], w=[("h", h)])
        if self.cfg.get("dbg"):
            P.op("sp", lambda e: e.dma_start(out=self.dbg, in_=self.hT[:].rearrange("p k n -> p (k n)")),
                 r=[("h", k) for k in range(KC)], w=["dbg"], dma="dbg")
            P.op("sp", None, r=["dbg"])
        if not state_only:
            self.out_proj("hg_out", 8, 4, [("h", k) for k in range(KC)], lambda k: self.hT[:, k, :])


V_C = 0
V_ADAB = V_C + KC
V_PRE = V_ADAB + 2 * 72
V_POST = V_PRE + 6 * KC
V_LB = V_POST + 6 * KC
V_ON = V_LB + 24
V_LNG = V_ON + 1
V_MASK = V_LNG + 24
NV = V_MASK + 8
C_ONES = 0
C_IDENT = 512
C_TRIL = 640
C_MNEG = 768
C_RMASK = 896
C_HM = 896 + 512
NCST = 896 + 512 + 2
SM_TMP = 0
SM_COND = 8
SM_LNG = 16
SM_MV = 40
SM_RV = 48
SM_EX = 56
SM_DEN = 80
SM_LB = 88
SM_C1 = 96
SM_C0 = 104
SM_GN = 112
SM_PTOT = 120
SM_PR = 128
SM_PM1 = 136


def pack_vecs(c_b, ada_b, norm_pre, norm_post, hg_lb, hg_out_norm, gm_ln_g, core):
    v = np.zeros((128, NV), np.float32)
    v[:, V_C:V_C + KC] = c_b.reshape(KC, 128).T
    for li in range(2):
        v[:, V_ADAB + li * 72:V_ADAB + (li + 1) * 72] = ada_b[li].reshape(72, 128).T
    v[:, V_PRE:V_PRE + 6 * KC] = norm_pre.reshape(6 * KC, 128).T
    v[:, V_POST:V_POST + 6 * KC] = norm_post.reshape(6 * KC, 128).T
    v[:, V_LB:V_LB + 24] = hg_lb.reshape(24, 128).T
    v[:, V_ON] = hg_out_norm.reshape(128)
    v[:, V_LNG:V_LNG + 24] = gm_ln_g.reshape(24, 128).T
    for j in range(NCORES):
        v[:, V_MASK + j] = 1.0 if (j // 4 == core // 4 and j < core) else 0.0
    return v


def make_consts():
    c = np.zeros((128, NCST), np.float32)
    c[:, C_ONES:C_ONES + 512] = 1.0
    c[:, C_IDENT:C_IDENT + 128] = np.eye(128, dtype=np.float32)
    s = np.arange(128)[:, None]
    t = np.arange(128)[None, :]
    c[:, C_TRIL:C_TRIL + 128] = (s <= t)
    c[:, C_MNEG:C_MNEG + 128] = -1.0 * ((s <= t) & (s // 64 == t // 64))
    c[:, C_RMASK:C_RMASK + 512] = (np.arange(512) % 64 != 0)[None, :]
    c[:64, C_HM] = 1.0
    c[64:, C_HM + 1] = 1.0
    return c


def make_gm_rows(gm_ln_b, gm_b_in, gm_b_s):
    row = np.concatenate([gm_ln_b.reshape(-1), gm_b_in.reshape(-1), gm_b_s.reshape(-1)]).astype(np.float32)
    return np.ascontiguousarray(np.broadcast_to(row[None, :], (3, row.size)))


def common_inputs(inp, core):
    b = core // 4
    return {
        "vecs": pack_vecs(inp["c"][b], inp["ada_b"], inp["norm_pre"], inp["norm_post"], inp["hg_lb"],
                          inp["hg_out_norm"], inp["gm_ln_g"], core),
        "cst": make_consts(),
        "ada_w": inp["ada_w"], "ffn_w_in": inp["ffn_w_in"], "ffn_w_out": inp["ffn_w_out"],
        "hg_w_in": inp["hg_w_in"], "hg_w_out": inp["hg_w_out"], "gm_w_in": inp["gm_w_in"], "gm_w_out": inp["gm_w_out"],
        "gm_rows": make_gm_rows(inp["gm_ln_b"], inp["gm_b_in"], inp["gm_b_s"]),
        "gm_wsT": np.ascontiguousarray(np.transpose(inp["gm_w_s"][0], (2, 0, 1))),
    }


FUSED = False
_NC_CACHE = {}


def _get_nc(mode, ntile):
    key = (mode, ntile)
    if key not in _NC_CACHE:
        _NC_CACHE[key] = Builder({"ntile": ntile, "mode": mode}).build()
    return _NC_CACHE[key]


def kernel(**inputs):
    inp = {k: np.ascontiguousarray(np.asarray(v, dtype=np.float32)) for k, v in inputs.items()}
    x = inp["x"]
    B, S, _ = x.shape
    seg = S // 4
    ntile = seg // T
    cores = list(range(NCORES))
    in_maps = []
    for core in cores:
        b, sg = core // 4, core % 4
        m = common_inputs(inp, core)
        m["xT"] = np.ascontiguousarray(x[b, sg * seg:(sg + 1) * seg].T)
        in_maps.append(m)
    if FUSED:
        res = run_bass_kernel_spmd(_get_nc("F", ntile), in_maps, core_ids=cores)
        outs = [r["yT"] for r in res.results]
    else:
        resA = run_bass_kernel_spmd(_get_nc("A", ntile), in_maps, core_ids=cores)
        Sall = np.ascontiguousarray(np.stack([np.asarray(r["Sloc"]) for r in resA.results]))
        Pall = np.ascontiguousarray(np.stack([np.asarray(r["Ploc"]) for r in resA.results]))
        maps2 = []
        for core in cores:
            m2 = dict(in_maps[core])
            m2["xT"] = np.ascontiguousarray(np.asarray(resA.results[core]["x1T"]))
            m2["Sall"] = Sall
            m2["Pall"] = Pall
            maps2.append(m2)
        resB = run_bass_kernel_spmd(_get_nc("B", ntile), maps2, core_ids=cores)
        outs = [r["yT"] for r in resB.results]
    y = np.empty((B, S, D), np.float32)
    for core in cores:
        b, sg = core // 4, core % 4
        y[b, sg * seg:(sg + 1) * seg] = np.asarray(outs[core]).T
    return y
```
